# Optimizing a Trainium2 kernel written in Bass

```python
import jax, jax.numpy as jnp
from jax import lax
import numpy as np

D_MODEL = 1024
BATCH = 2
SEQ = 8192
DEPTH = 2
DEC_BATCH = 8
DEC_SEQ = 64
PAST_LEN = 2048

CHUNK = 64
N_SB_HEADS = 8
SB_HEAD_DIM = D_MODEL // N_SB_HEADS
D_SB = N_SB_HEADS * SB_HEAD_DIM
SB_BLOCK = 128
N_SG_GROUPS = 4
D_SG = D_MODEL
SG_GROUP_DIM = D_SG // N_SG_GROUPS
SG_CHUNK = 128
N_MEM = 256
N_MEM_HEADS = 4
MEM_HEAD_DIM = D_MODEL // N_MEM_HEADS
D_MEMQ = N_MEM_HEADS * MEM_HEAD_DIM
N_BRANCH = 3
D_FF = -(-8 * D_MODEL // (3 * 256)) * 256
IN_SIZES = (D_SB, D_SB, D_SB, D_SG, D_SG, D_MEMQ)
D_IN = sum(IN_SIZES) + N_BRANCH * D_MODEL
EPS = 1e-6

kernel_name = "stickbreak_gmlp_memory_stream_encoder"


def rmsnorm(x, g):
    xf = x.astype(jnp.float32)
    y = xf * lax.rsqrt(jnp.mean(xf * xf, axis=-1, keepdims=True) + EPS)
    return (y * g.astype(jnp.float32)).astype(x.dtype)


def layernorm(x, g, b):
    xf = x.astype(jnp.float32)
    mu = jnp.mean(xf, axis=-1, keepdims=True)
    var = jnp.mean(jnp.square(xf - mu), axis=-1, keepdims=True)
    y = (xf - mu) * lax.rsqrt(var + EPS)
    return (y * g.astype(jnp.float32) + b.astype(jnp.float32)).astype(x.dtype)


def stick_breaking(q, k, v, q_pos, k_pos):
    z = jnp.einsum('bqhd,bkhd->bhqk', q.astype(jnp.float32), k.astype(jnp.float32)) * (SB_HEAD_DIM ** -0.5)
    mask = k_pos[None, :] < q_pos[:, None]
    log_keep = jnp.where(mask, jax.nn.log_sigmoid(-z), 0.0)
    between = lax.cumsum(log_keep, axis=3, reverse=True) - log_keep
    w = jnp.where(mask, jnp.exp(jax.nn.log_sigmoid(z) + between), 0.0)
    return jnp.einsum('bhqk,bkhd->bqhd', w, v.astype(jnp.float32)).astype(v.dtype)


def sb_prompt(q, k, v):
    b, s, h, d = q.shape
    nb = s // SB_BLOCK
    qb = q.reshape(b, nb, SB_BLOCK, h, d).transpose(1, 0, 2, 3, 4)
    k_pos = jnp.arange(s)

    def block(args):
        q_blk, i = args
        q_pos = i * SB_BLOCK + jnp.arange(SB_BLOCK)
        return stick_breaking(q_blk, k, v, q_pos, k_pos)

    out = lax.map(block, (qb, jnp.arange(nb)))
    return out.transpose(1, 0, 2, 3, 4).reshape(b, s, h, d)


def sb_sample(q, k, v, cache_k, cache_v):
    t = q.shape[1]
    k_all = jnp.concatenate([cache_k.astype(k.dtype), k], axis=1)
    v_all = jnp.concatenate([cache_v.astype(v.dtype), v], axis=1)
    q_pos = PAST_LEN + jnp.arange(t)
    k_pos = jnp.arange(PAST_LEN + t)
    return stick_breaking(q, k_all, v_all, q_pos, k_pos)


def spatial_prompt(v, w_s, b_s):
    b, s, _ = v.shape
    vb = v.reshape(b, s // SG_CHUNK, SG_CHUNK, N_SG_GROUPS, SG_GROUP_DIM)
    w = jnp.tril(w_s).astype(v.dtype)
    out = jnp.einsum('gts,bnsgc->bntgc', w, vb) + b_s.T.astype(v.dtype)[None, None, :, :, None]
    return out.reshape(b, s, D_SG)


def spatial_sample(v, w_s, b_s):
    b, t, _ = v.shape
    vg = v.reshape(b, t, N_SG_GROUPS, SG_GROUP_DIM)
    w = jnp.tril(w_s)[:, :t, :t].astype(v.dtype)
    out = jnp.einsum('gts,bsgc->btgc', w, vg) + b_s[:, :t].T.astype(v.dtype)[None, :, :, None]
    return out.reshape(b, t, D_SG)


def memory_kv(mem, g_mem, w_mem_kv):
    b = mem.shape[0]
    kv = rmsnorm(mem, g_mem) @ w_mem_kv
    k, v = jnp.split(kv, 2, axis=-1)
    return (k.reshape(b, N_MEM, N_MEM_HEADS, MEM_HEAD_DIM), v.reshape(b, N_MEM, N_MEM_HEADS, MEM_HEAD_DIM))


def cross_attend(q, k, v):
    s = jnp.einsum('bshd,bmhd->bhsm', q.astype(jnp.float32), k.astype(jnp.float32)) * (MEM_HEAD_DIM ** -0.5)
    p = jax.nn.softmax(s, axis=-1)
    return jnp.einsum('bhsm,bmhd->bshd', p, v.astype(jnp.float32)).astype(q.dtype)


def mixer_sublayer(x, sb_fn, sg_fn, mem_k, mem_v, g_pre, w_in, b_gate, ln_g, ln_b, w_branch, w_out, g_post):
    lead = x.shape[:2]
    h = rmsnorm(x, g_pre)
    cuts = [int(c) for c in np.cumsum(IN_SIZES)]
    q_sb, k_sb, v_sb, u_sg, v_sg, q_mem, z_gate = jnp.split(h @ w_in, cuts, axis=-1)
    q_sb = q_sb.reshape(*lead, N_SB_HEADS, SB_HEAD_DIM)
    k_sb = k_sb.reshape(*lead, N_SB_HEADS, SB_HEAD_DIM)
    v_sb = v_sb.reshape(*lead, N_SB_HEADS, SB_HEAD_DIM)
    o_sb = sb_fn(q_sb, k_sb, v_sb).reshape(*lead, D_SB)
    v_n = layernorm(jax.nn.gelu(v_sg), ln_g, ln_b)
    o_sg = jax.nn.gelu(u_sg) * sg_fn(v_n)
    o_mem = cross_attend(q_mem.reshape(*lead, N_MEM_HEADS, MEM_HEAD_DIM), mem_k, mem_v).reshape(*lead, D_MEMQ)
    gates = jax.nn.sigmoid((z_gate + b_gate).astype(jnp.float32)).astype(x.dtype)
    gates = gates.reshape(*lead, N_BRANCH, D_MODEL)
    merged = sum(gates[..., i, :] * (o @ w_branch[i]) for i, o in enumerate((o_sb, o_sg, o_mem)))
    return x + rmsnorm(merged @ w_out, g_post), k_sb, v_sb, v_n


def ffn_sublayer(x, g_pre, w_ffn_in, w_ffn_out, g_post):
    a, b = jnp.split(rmsnorm(x, g_pre) @ w_ffn_in, 2, axis=-1)
    return x + rmsnorm((jax.nn.silu(a) * b) @ w_ffn_out, g_post)


def setup_inputs(seed: int = 0) -> dict:
    key = jax.random.key(seed)
    ks = iter(jax.random.split(key, 32))
    nrm = lambda shape, scale: jax.random.normal(next(ks), shape, jnp.float32) * scale
    gain = lambda shape: 1.0 + nrm(shape, 0.01)
    return {
        "x_prompt": nrm((BATCH, SEQ, D_MODEL), 1.0),
        "x_sample": nrm((DEC_BATCH, DEC_SEQ, D_MODEL), 1.0),
        "cache_sb_k": nrm((DEPTH, DEC_BATCH, PAST_LEN, N_SB_HEADS, SB_HEAD_DIM), 1.0),
        "cache_sb_v": nrm((DEPTH, DEC_BATCH, PAST_LEN, N_SB_HEADS, SB_HEAD_DIM), 1.0),
        "cache_mem_k": nrm((DEPTH, DEC_BATCH, N_MEM, N_MEM_HEADS, MEM_HEAD_DIM), 1.0),
        "cache_mem_v": nrm((DEPTH, DEC_BATCH, N_MEM, N_MEM_HEADS, MEM_HEAD_DIM), 1.0),
        "mem_prompt": nrm((BATCH, N_MEM, D_MODEL), 1.0),
        "g_pre_mix": gain((DEPTH, D_MODEL)),
        "w_in": nrm((DEPTH, D_MODEL, D_IN), D_MODEL ** -0.5),
        "b_gate": nrm((DEPTH, N_BRANCH * D_MODEL), 0.01),
        "ln_sg_g": gain((DEPTH, D_SG)),
        "ln_sg_b": nrm((DEPTH, D_SG), 0.01),
        "w_spatial": nrm((DEPTH, N_SG_GROUPS, SG_CHUNK, SG_CHUNK), SG_CHUNK ** -0.5),
        "b_spatial": gain((DEPTH, N_SG_GROUPS, SG_CHUNK)),
        "g_mem": gain((DEPTH, D_MODEL)),
        "w_mem_kv": nrm((DEPTH, D_MODEL, 2 * D_MEMQ), D_MODEL ** -0.5),
        "w_branch": nrm((DEPTH, N_BRANCH, D_MODEL, D_MODEL), D_MODEL ** -0.5),
        "w_out": nrm((DEPTH, D_MODEL, D_MODEL), D_MODEL ** -0.5),
        "g_post_mix": gain((DEPTH, D_MODEL)),
        "g_pre_ffn": gain((DEPTH, D_MODEL)),
        "w_ffn_in": nrm((DEPTH, D_MODEL, 2 * D_FF), D_MODEL ** -0.5),
        "w_ffn_out": nrm((DEPTH, D_FF, D_MODEL), D_FF ** -0.5),
        "g_post_ffn": gain((DEPTH, D_MODEL)),
    }


def reference(x_prompt, x_sample, cache_sb_k, cache_sb_v, cache_mem_k, cache_mem_v, mem_prompt,
              g_pre_mix, w_in, b_gate, ln_sg_g, ln_sg_b, w_spatial, b_spatial, g_mem, w_mem_kv,
              w_branch, w_out, g_post_mix, g_pre_ffn, w_ffn_in, w_ffn_out, g_post_ffn):
    xp, xs = x_prompt, x_sample
    sbk_p, sbv_p, mk_p, mv_p, sbk_s, sbv_s, sgv_s = [], [], [], [], [], [], []
    for l in range(DEPTH):
        mem_k, mem_v = memory_kv(mem_prompt, g_mem[l], w_mem_kv[l])
        xp, k_new, v_new, _ = mixer_sublayer(
            xp, sb_prompt, lambda v: spatial_prompt(v, w_spatial[l], b_spatial[l]), mem_k, mem_v,
            g_pre_mix[l], w_in[l], b_gate[l], ln_sg_g[l], ln_sg_b[l], w_branch[l], w_out[l], g_post_mix[l])
        xp = ffn_sublayer(xp, g_pre_ffn[l], w_ffn_in[l], w_ffn_out[l], g_post_ffn[l])
        sbk_p.append(k_new); sbv_p.append(v_new); mk_p.append(mem_k); mv_p.append(mem_v)
        xs, k_new, v_new, sg_v = mixer_sublayer(
            xs, lambda q, k, v: sb_sample(q, k, v, cache_sb_k[l], cache_sb_v[l]),
            lambda v: spatial_sample(v, w_spatial[l], b_spatial[l]), cache_mem_k[l], cache_mem_v[l],
            g_pre_mix[l], w_in[l], b_gate[l], ln_sg_g[l], ln_sg_b[l], w_branch[l], w_out[l], g_post_mix[l])
        xs = ffn_sublayer(xs, g_pre_ffn[l], w_ffn_in[l], w_ffn_out[l], g_post_ffn[l])
        sbk_s.append(k_new); sbv_s.append(v_new); sgv_s.append(sg_v)
    return (xp, xs, jnp.stack(sbk_p), jnp.stack(sbv_p), jnp.stack(mk_p), jnp.stack(mv_p),
            jnp.stack(sbk_s), jnp.stack(sbv_s), jnp.stack(sgv_s))
```

```python
import numpy as np
import ml_dtypes
import concourse.bass as bass
import concourse.mybir as mybir
from concourse.bass_utils import run_bass_kernel_spmd

F32 = mybir.dt.float32
BF16 = mybir.dt.bfloat16
AF = mybir.ActivationFunctionType
ALU = mybir.AluOpType

DEPTH = 2
NQ = 16
EPS = 1e-6
GC1 = 0.7978845608028654
GC2 = 0.044715

import os
STOP = os.environ.get('K_STOP')


def _isz(dt):
    return 4 if dt == F32 else 2


class Acc:
    __slots__ = ("ap", "space", "lo", "hi")

    def __init__(self, ap, space, lo, hi):
        self.ap = ap
        self.space = space
        self.lo = lo
        self.hi = hi


class TT:
    def __init__(self, space, handle, base, shape, dt):
        self.space = space
        self.h = handle
        self.base = base
        self.shape = list(shape)
        self.dt = dt
        self.isz = _isz(dt)
        st = [1] * len(shape)
        for i in range(len(shape) - 2, -1, -1):
            st[i] = st[i + 1] * shape[i + 1]
        self.st = st

    def __getitem__(self, idx):
        if not isinstance(idx, tuple):
            idx = (idx,)
        idx = list(idx) + [slice(None)] * (len(self.shape) - len(idx))
        lo = 0
        hi = 0
        for d in range(1, len(self.shape)):
            i = idx[d]
            if isinstance(i, slice):
                a = 0 if i.start is None else i.start
                b = self.shape[d] if i.stop is None else i.stop
            else:
                a, b = i, i + 1
            assert 0 <= a < b <= self.shape[d], (self.shape, idx)
            lo += a * self.st[d]
            hi += (b - 1) * self.st[d]
        hi += 1
        if self.space[0] == "d":
            i = idx[0]
            if isinstance(i, slice):
                a = 0 if i.start is None else i.start
                b = self.shape[0] if i.stop is None else i.stop
            else:
                a, b = i, i + 1
            lo += a * self.st[0]
            hi += (b - 1) * self.st[0]
        return Acc(self.h[tuple(idx)], self.space, self.base + lo * self.isz, self.base + hi * self.isz)

    def whole(self, ap=None):
        n = 1
        for s in (self.shape if self.space[0] == "d" else self.shape[1:]):
            n *= s
        return Acc(self.h[:] if ap is None else ap, self.space, self.base, self.base + n * self.isz)


class Op:
    __slots__ = ("eng", "fn", "deps", "dep", "sem", "val", "kind")

    def __init__(self, eng, fn, kind):
        self.eng = eng
        self.fn = fn
        self.deps = {}
        self.dep = False
        self.sem = None
        self.val = 0
        self.kind = kind


ENGS = ("tensor", "vector", "scalar", "gpsimd", "sync")
SAME_ENGINE_SYNC = {"tensor": False, "vector": True, "scalar": True, "gpsimd": True, "sync": True}
EPOCH = 2000


class Sched:
    def __init__(self, nc):
        self.nc = nc
        self.ops = {e: [] for e in ENGS}
        self.n_dma_sems = {"sync": 16, "gpsimd": 8, "scalar": 4}
        self.wr = {}
        self.rd = {}

    dry = False

    def add(self, eng, fn, reads=(), writes=(), kind="c"):
        if self.dry:
            return None
        op = Op(eng, fn, kind)
        if kind == "d":
            op.dep = True
        deps = op.deps
        for a in reads:
            for rec in self.wr.get(a.space, ()):
                if rec[0] < a.hi and a.lo < rec[1]:
                    deps[id(rec[2])] = rec[2]
        for a in writes:
            wl = self.wr.setdefault(a.space, [])
            keep = []
            for rec in wl:
                if rec[0] < a.hi and a.lo < rec[1]:
                    deps[id(rec[2])] = rec[2]
                    if a.lo <= rec[0] and rec[1] <= a.hi:
                        continue
                keep.append(rec)
            self.wr[a.space] = keep
            rl = self.rd.setdefault(a.space, [])
            keep = []
            for rec in rl:
                if rec[0] < a.hi and a.lo < rec[1]:
                    deps[id(rec[2])] = rec[2]
                    if a.lo <= rec[0] and rec[1] <= a.hi:
                        continue
                keep.append(rec)
            self.rd[a.space] = keep
        for d in list(deps.values()):
            if d.kind == "c" and kind == "c" and d.eng == eng and not SAME_ENGINE_SYNC[eng]:
                del deps[id(d)]
                continue
            d.dep = True
        for a in reads:
            rl = self.rd.setdefault(a.space, [])
            if kind == "c":
                for i, rec in enumerate(rl):
                    if rec[0] == a.lo and rec[1] == a.hi and rec[2].eng == eng and rec[2].kind == "c":
                        rl[i] = (a.lo, a.hi, op)
                        break
                else:
                    rl.append((a.lo, a.hi, op))
            else:
                rl.append((a.lo, a.hi, op))
        for a in writes:
            self.wr[a.space].append((a.lo, a.hi, op))
        self.ops[eng].append(op)
        return op

    def mm(self, out, lhsT, rhs, start=True, stop=True):
        rd = [lhsT, rhs] + ([] if start else [out])
        return self.add("tensor", lambda e: e.matmul(out.ap, lhsT=lhsT.ap, rhs=rhs.ap, start=start, stop=stop), rd, [out])

    def tr(self, out, in_, ident):
        return self.add("tensor", lambda e: e.transpose(out.ap, in_.ap, ident.ap), [in_, ident], [out])

    def act(self, out, in_, func, bias=None, scale=None, accum_out=None):
        rd = [in_]
        kw = {}
        if bias is not None:
            if isinstance(bias, Acc):
                rd.append(bias)
                kw["bias"] = bias.ap
            else:
                kw["bias"] = float(bias)
        if scale is not None:
            if isinstance(scale, Acc):
                rd.append(scale)
                kw["scale"] = scale.ap
            else:
                kw["scale"] = float(scale)
        wr = [out]
        if accum_out is not None:
            kw["accum_out"] = accum_out.ap
            wr.append(accum_out)
        return self.add("scalar", lambda e: e.activation(out=out.ap, in_=in_.ap, func=func, **kw), rd, wr)

    def tt(self, eng, out, in0, in1, op):
        return self.add(eng, lambda e: e.tensor_tensor(out=out.ap, in0=in0.ap, in1=in1.ap, op=op), [in0, in1], [out])

    def ts(self, eng, out, in0, s1, s2, op0, op1=None):
        rd = [in0]
        a1 = s1.ap if isinstance(s1, Acc) else s1
        a2 = s2.ap if isinstance(s2, Acc) else s2
        if isinstance(s1, Acc):
            rd.append(s1)
        if isinstance(s2, Acc):
            rd.append(s2)
        if op1 is None:
            return self.add(eng, lambda e: e.tensor_scalar(out=out.ap, in0=in0.ap, scalar1=a1, scalar2=None, op0=op0), rd, [out])
        return self.add(eng, lambda e: e.tensor_scalar(out=out.ap, in0=in0.ap, scalar1=a1, scalar2=a2, op0=op0, op1=op1), rd, [out])

    def stt(self, out, in0, scalar, in1, op0, op1):
        rd = [in0, in1]
        sc = scalar.ap if isinstance(scalar, Acc) else scalar
        if isinstance(scalar, Acc):
            rd.append(scalar)
        return self.add("vector", lambda e: e.scalar_tensor_tensor(out=out.ap, in0=in0.ap, scalar=sc, in1=in1.ap, op0=op0, op1=op1), rd, [out])

    def cp(self, eng, out, in_):
        if eng == "scalar":
            return self.add(eng, lambda e: e.copy(out=out.ap, in_=in_.ap), [in_], [out])
        return self.add(eng, lambda e: e.tensor_copy(out=out.ap, in_=in_.ap), [in_], [out])

    def dma(self, q, out, in_, extra_reads=()):
        return self.add(q, lambda e: e.dma_start(out=out.ap, in_=in_.ap), [in_] + list(extra_reads), [out], kind="d")

    def wait_all(self, eng, accs):
        return self.add(eng, None, [], accs, kind="c")

    def run_block(self):
        nc = self.nc
        plans = {}
        ccsem = [None, 0]
        for e in ENGS:
            ops = self.ops[e]
            if not ops:
                continue
            cur = None
            cnt = 0
            ep = 0
            pool = None
            pool_last = None
            pi = 0
            plan = []
            for op in ops:
                pre = []
                if op.dep:
                    if op.kind == "d":
                        if pool is None:
                            pool = [nc.alloc_semaphore(name=f"d_{e}_{i}") for i in range(self.n_dma_sems[e])]
                            pool_last = [0] * len(pool)
                        k = pi % len(pool)
                        pi += 1
                        if pool_last[k] > 0:
                            pre.append((pool[k], pool_last[k]))
                        pool_last[k] += 16
                        op.sem, op.val = pool[k], pool_last[k]
                    elif op.kind == "cc":
                        if ccsem[0] is None:
                            ccsem[0] = [[nc.alloc_semaphore(name=f"cc_sem{i}"), 0] for i in range(4)]
                        slot = ccsem[0][ccsem[1] % 4]
                        ccsem[1] += 1
                        slot[1] += 1
                        op.sem, op.val = slot[0], slot[1]
                    else:
                        if cur is None or cnt >= EPOCH:
                            cur = nc.alloc_semaphore(name=f"e_{e}_{ep}")
                            ep += 1
                            cnt = 0
                        cnt += 1
                        op.sem, op.val = cur, cnt
                plan.append((op, pre))
            plans[e] = plan
        stats = {}

        def mk(plan, e):
            def body(eng):
                waited = {}
                nw = 0
                for op, pre in plan:
                    best = {}
                    for s, v in pre + [(d.sem, d.val) for d in op.deps.values()]:
                        k = s.num
                        if waited.get(k, 0) >= v:
                            continue
                        if k not in best or best[k][1] < v:
                            best[k] = (s, v)
                    for k, (s, v) in best.items():
                        eng.wait_ge(s, v)
                        waited[k] = v
                        nw += 1
                    if op.fn is None:
                        continue
                    ins = op.fn(eng)
                    if op.dep:
                        ins.then_inc(op.sem, 16 if op.kind == "d" else 1)
                stats[e] = (len(plan), nw)
            return body

        with nc.Block() as block:
            for e, plan in plans.items():
                getattr(block, e)(mk(plan, e))
        return stats


NCB = 768 + 2048


def host_consts():
    cb = np.zeros((128, NCB), np.float32)
    j = np.arange(128)[:, None]
    s = np.arange(128)[None, :]
    cb[:, 0:128] = (j == s)
    cb[:, 128:256] = -(j >= s).astype(np.float32)
    cb[:, 256:384] = -1.0
    cb[:, 384:512] = 1.0
    cb[:, 640:768] = (j <= s)
    t = np.arange(512)[None, :]
    for m in range(4):
        cb[:, 768 + m * 512: 768 + (m + 1) * 512] = ((m * 128 + j) < t)
    return cb.astype(ml_dtypes.bfloat16), np.eye(128, dtype=np.float32)


def build():
    nc = bass.Bass("TRN2", target_bir_lowering=False)
    S = Sched(nc)
    names = set()

    def uniq(n):
        i = 0
        m = n
        while m in names:
            i += 1
            m = f"{n}_{i}"
        names.add(m)
        return m

    def din(name, shape, dt=F32):
        h = nc.dram_tensor(name, shape, dt, kind="ExternalInput")
        return TT(("d", name), h.ap(), 0, shape, dt)

    def dout(name, shape, dt=F32):
        h = nc.dram_tensor(name, shape, dt, kind="ExternalOutput")
        return TT(("d", name), h.ap(), 0, shape, dt)

    def dint(name, shape, dt):
        h = nc.dram_tensor(name, shape, dt, kind="Internal", addr_space="Local")
        return TT(("d", name), h.ap(), 0, shape, dt)

    def RO(ap):
        return Acc(ap, ("x",), 0, 0)

    def DA(t, ap, lo, hi):
        return Acc(ap, t.space, lo * t.isz, hi * t.isz)

    xp = din("xp", [2048, 1024]); xs = din("xs", [64, 1024])
    csk = din("csk", [2, 2048, 1024]); csv = din("csv", [2, 2048, 1024])
    cmk = din("cmk", [2, 256, 1024]); cmv = din("cmv", [2, 256, 1024])
    memp = din("memp", [256, 1024])
    w_in = din("w_in", [2, 1024, 9216]); w_own = din("w_own", [2, 2, 1024, 384])
    w_mkv = din("w_mkv", [2, 1024, 2048]); w_br = din("w_br", [2, 3, 1024, 1024])
    w_o = din("w_o", [2, 1024, 1024]); w_f1 = din("w_f1", [2, 1024, 5632]); w_f2 = din("w_f2", [2, 2816, 1024])
    cols = din("cols", [2, 128, 64]); lnp = din("lnp", [2, 2, 1024])
    wsT = din("wsT", [2, 128, 512]); bsp = din("bsp", [2, 512])
    cbd = din("cb", [128, NCB], BF16); cfd = din("cf", [128, 128]); seld = din("sel", [128, 4])

    y_p = dout("y_p", [2048, 1024]); y_s = dout("y_s", [64, 1024])
    sbk_p = dout("sbk_p", [2, 8192, 256]); sbv_p = dout("sbv_p", [2, 8192, 256])
    mk_p = dout("mk_p", [2, 256, 1024]); mv_p = dout("mv_p", [2, 256, 1024])
    sbk_s = dout("sbk_s", [2, 64, 1024]); sbv_s = dout("sbv_s", [2, 64, 1024]); sgv_s = dout("sgv_s", [2, 64, 1024])
    outs_all = [y_p, y_s, sbk_p, sbv_p, mk_p, mv_p, sbk_s, sbv_s, sgv_s]

    hb = dint("hb", [4096, 512], BF16); hall = dint("gall", [16384, 512], BF16)
    ob = dint("ob", [1024, 2048], BF16)
    oall = TT(hall.space, hall.h.rearrange("(a b) c -> a (b c)", b=4), 0, [4096, 2048], BF16)

    SB_LO = ((nc.sbuf_base + 63) // 64) * 64
    SB_HI = nc.sbuf_top

    class Arena:
        def __init__(s, lo, hi):
            s.lo, s.hi, s.p = lo, hi, lo

        def alloc(s, name, shape, dt):
            n = _isz(dt)
            for d in shape[1:]:
                n *= d
            off = ((s.p + 63) // 64) * 64
            assert off + n <= s.hi, f"SBUF arena overflow at {name}: need {off + n - s.lo} > {s.hi - s.lo}"
            s.p = off + n
            h = nc.alloc_sbuf_tensor_at(uniq(name), list(shape), dt, offset=off)
            return TT(("s",), h, off, shape, dt)

        def sub(s):
            return Arena(s.p, s.hi)

    A0 = Arena(SB_LO, SB_HI)
    XT = A0.alloc("XT", [128, 8, 2048], F32)
    XTs = A0.alloc("XTs", [128, 8, 64], F32)
    HTs = A0.alloc("HTs", [128, 8, 64], BF16)
    CB = A0.alloc("CB", [128, NCB], BF16)
    CF = A0.alloc("CF", [128, 128], F32)
    COLS = A0.alloc("COLS", [128, 2, 64], F32)
    BGH = A0.alloc("BGH", [128, 2, 24], F32)
    FENCE = A0.alloc("FENCE", [128, 8], F32)
    ONES1 = A0.alloc("ONES1", [1, 128], F32)
    SEL = A0.alloc("SEL", [128, 4], F32)
    ident = CB[:, 0:128]; negU = CB[:, 128:256]; negOnes = CB[:, 256:384]; ones = CB[:, 384:512]
    zeros = CB[:, 512:640]; mincl = CB[:, 640:768]

    def maskm(m, nk=128, w=512):
        return CB[0:nk, 768 + m * 512: 768 + m * 512 + w]

    psh = [nc.alloc_psum_tensor(f"ps{i}", [128, 512], F32) for i in range(8)]
    PS = [TT(("p",), psh[i], i * 2048, [128, 512], F32) for i in range(8)]
    PSB = [TT(("p",), psh[i].bitcast(BF16), i * 2048, [128, 1024], BF16) for i in range(8)]
    rr = {"i": 0}

    def nb(lo=0, hi=8):
        rr["i"] = (rr["i"] + 1) % (hi - lo)
        return lo + rr["i"]

    S.dma("sync", CB[:, :], RO(cbd.h))
    S.dma("sync", CF[:, :], RO(cfd.h))
    S.dma("sync", SEL[:, :], RO(seld.h))
    S.dma("sync", COLS[:, :, :], RO(cols.h.rearrange("l p c -> p l c")))
    S.ts("vector", BGH[:, :, :], COLS[:, :, 40:64], 0.5, None, ALU.mult)
    S.add("vector", lambda e: e.memset(ONES1[0:1, :].ap, 1.0), [], [ONES1[0:1, :]])

    def gcol(l, which, k):
        return COLS[:, l, which * 8 + k: which * 8 + k + 1]

    def rstd_from_ss(ss, out, tmp, T, npart=128):
        S.act(tmp, ss, AF.Ln, bias=EPS, scale=1.0 / 1024)
        S.act(out, tmp, AF.Exp, scale=-0.5)

    def rms_fm(src_k, gc_k, out_k, T, sq, rstd, tmp):
        b = nb()
        for k in range(8):
            S.act(sq[k % 2], src_k(k), AF.Square)
            S.mm(PS[b][:, 0:T], ones, sq[k % 2], start=(k == 0), stop=(k == 7))
        rstd_from_ss(PS[b][:, 0:T], rstd, tmp, T)
        for k in range(8):
            S.stt(out_k(k), src_k(k), gc_k(k), rstd, ALU.mult, ALU.mult)

    A1 = A0.sub()

    def rv(acc, pattern, **kw):
        return Acc(acc.ap.rearrange(pattern, **kw), acc.space, acc.lo, acc.hi)

    GRP = [[0, 1, 2, 3], [4, 5, 6, 7]]

    def allgather(src, dst, q, rows):
        a = src[q * rows:(q + 1) * rows, :]
        b = dst[q * 4 * rows:(q + 1) * 4 * rows, :]
        S.add("gpsimd", lambda e: e.memset(FENCE[:, :].ap, 0.0), [a, b], [FENCE[:, :], b])
        S.add("gpsimd", lambda e: e.collective_compute("AllGather", ALU.bypass, replica_groups=GRP,
                                                        ins=[a.ap], outs=[b.ap]),
              [a], [b], kind="cc")

    def load_x():
        P = A1.sub()
        stg = [P.alloc("xstg", [128, 1024], F32) for _ in range(2)]
        for c in range(17):
            st = stg[c % 2]
            n = 128 if c < 16 else 64
            src = xp.h[c * 128:(c + 1) * 128, :] if c < 16 else xs.h
            S.dma("sync", st[0:n, :], RO(src))
            for half in range(2):
                b = nb()
                for kk in range(4):
                    k = half * 4 + kk
                    S.tr(PS[b][:, kk * n:(kk + 1) * n], st[0:n, k * 128:(k + 1) * 128], CF[0:n, 0:n])
                src_ps = rv(PS[b][:, 0:4 * n], "p (k t) -> p k t", k=4)
                dst = XT[:, half * 4:half * 4 + 4, c * 128:(c + 1) * 128] if c < 16 else XTs[:, half * 4:half * 4 + 4, :]
                S.cp("vector" if half == 0 else "scalar", dst, src_ps)

    def s1(l):
        P = A1.sub()
        sq = [P.alloc("sq", [128, 512], BF16) for _ in range(2)]
        rstd = P.alloc("rstd", [128, 512], F32)
        tmp = P.alloc("tmp", [128, 512], F32)
        HT = [P.alloc("HT", [128, 8, 512], BF16) for _ in range(2)]
        for tt in range(4):
            h = HT[tt % 2]
            c0 = tt * 512
            rms_fm(lambda k: XT[:, k, c0:c0 + 512], lambda k: gcol(l, 0, k), lambda k: h[:, k, :], 512,
                   [s_[:, :] for s_ in sq], rstd[:, :], tmp[:, :])
            d_ = hb[tt * 1024:(tt + 1) * 1024, :]
            S.dma("sync", rv(d_, "(k p) n -> p k n", p=128), h[:, :, :])
            allgather(hb, hall, tt, 1024)
        rms_fm(lambda k: XTs[:, k, :], lambda k: gcol(l, 0, k), lambda k: HTs[:, k, :], 64,
               [s_[:, 0:64] for s_ in sq], rstd[:, 0:64], tmp[:, 0:64])

    def attn_bufs(P):
        return dict(
            e=[P.alloc("ae", [128, 512], F32) for _ in range(2)],
            sp=[P.alloc("asp", [128, 512], BF16) for _ in range(3)],
            w=[P.alloc("aw", [128, 512], BF16) for _ in range(2)],
            R=[P.alloc("aR", [128, 512], BF16) for _ in range(2)],
        )

    def attn_pipeline(tiles, B):
        n = len(tiles)
        ZB = [0, 1, 2, 3]

        def stA(t):
            T = tiles[t]
            z = PS[ZB[t % 4]]
            multi = len(T["zmm"]) > 1
            if multi:
                S.mm(z[0:T["nk"], 0:T["W"]], CB[:, 512:512 + T["nk"]], CB[:, 0:T["W"]], True, False)
            for (c0, c1, lhsT, rhs) in T["zmm"]:
                S.mm(z[0:T["nk"], c0:c1], lhsT, rhs, not multi, True)

        def stB(t):
            T = tiles[t]
            nk, W = T["nk"], T["W"]
            z = PS[ZB[t % 4]][0:nk, 0:W]
            e = B["e"][t % 2][0:nk, 0:W]
            sp = B["sp"][t % 3][0:nk, 0:W]
            S.act(e, z, AF.Exp)
            S.act(sp, e, AF.Ln, bias=1.0)
            for (c0, c1, m) in T["mask"]:
                a = B["sp"][t % 3][0:nk, c0:c1]
                S.tt("vector", a, a, m, ALU.mult)

        def stC(t):
            T = tiles[t]
            nk, W = T["nk"], T["W"]
            z = PS[ZB[t % 4]][0:nk, 0:W]
            sp = B["sp"][t % 3][0:nk, 0:W]
            S.add("tensor", lambda e: e.matmul(z.ap, lhsT=CB[0:nk, 128:128 + nk].ap, rhs=sp.ap, start=False, stop=T["first"],
                                               skip_group_check=True),
                  [CB[0:nk, 128:128 + nk], sp, z], [z])
            if not T["first"]:
                Rc = B["R"][T["ri"] % 2][:, 0:W]
                S.add("tensor", lambda e: e.matmul(z.ap, lhsT=CB[:, 256:256 + nk].ap, rhs=Rc.ap, start=False, stop=True,
                                                   skip_group_check=True),
                      [CB[:, 256:256 + nk], Rc, z], [z])
            if not T["last"]:
                Rn = B["R"][(T["ri"] + 1) % 2]
                if T["first"]:
                    if nk < 128:
                        S.add("gpsimd", lambda e: e.memset(Rn[:, 0:W].ap, 0.0), [], [Rn[:, 0:W]])
                    S.cp("gpsimd", Rn[0:nk, 0:W], sp)
                else:
                    S.tt("gpsimd", Rn[:, 0:W], B["R"][T["ri"] % 2][:, 0:W], sp, ALU.add)

        def stD(t):
            T = tiles[t]
            nk, W = T["nk"], T["W"]
            z = PS[ZB[t % 4]][0:nk, 0:W]
            w = B["w"][t % 2][0:nk, 0:W]
            S.act(w, z, AF.Exp)
            for (c0, c1, m) in T["mask"]:
                a = B["w"][t % 2][0:nk, c0:c1]
                S.tt("vector", a, a, m, ALU.mult)

        def stE(t):
            T = tiles[t]
            nk = T["nk"]
            multi = len(T["av"]) > 1
            if multi and T["first"]:
                S.mm(PS[T["ob"]][:, 0:T["W"]], CB[:, 512:640], CB[:, 0:T["W"]], True, False)
            for (out, lhsT, c0, c1) in T["av"]:
                S.mm(out, lhsT, B["w"][t % 2][0:nk, c0:c1], T["first"] and not multi, T["last"])
            if T["post"] is not None:
                T["post"]()

        for step in range(-2, n):
            if 0 <= step:
                stC(step)
            if step + 2 < n:
                stA(step + 2)
            if 0 <= step + 1 < n:
                stB(step + 1)
            if 0 <= step:
                stD(step)
                stE(step)

    def s23(l, hh):
        P = A1.sub()
        QT = P.alloc("QT", [128, 8192], BF16)
        KT = P.alloc("KT", [128, 8192], BF16)
        V = P.alloc("V", [128, 64, 128], BF16)
        WO = P.alloc("WO", [128, 8, 384], BF16)
        HTt = [P.alloc("HTt", [128, 8, 512], BF16) for _ in range(2)]
        stg = [P.alloc("kvstg", [128, 4, 256], F32) for _ in range(2)]
        oT = [P.alloc("oT", [128, 512], BF16) for _ in range(2)]
        AB = attn_bufs(P)
        S.dma("gpsimd", WO[:, :, :], RO(w_own.h[l, hh].rearrange("(k p) n -> p k n", p=128)))
        it = 0
        for r in range(4):
            for tt in range(4):
                g0 = r * 2048 + tt * 512
                h = HTt[it % 2]
                st = stg[it % 2]
                it += 1
                S.dma("sync", h[:, :, :], rv(hall[tt * 4096 + r * 1024:tt * 4096 + (r + 1) * 1024, :], "(k p) n -> p k n", p=128))
                b = nb(4, 8)
                for k in range(8):
                    S.mm(PS[b][:, :], WO[:, k, 0:128], h[:, k, :], k == 0, k == 7)
                S.act(QT[:, g0:g0 + 512], PS[b][:, :], AF.Copy, scale=128 ** -0.5)
                b = nb(4, 8)
                for k in range(8):
                    S.mm(PS[b][:, :], WO[:, k, 128:256], h[:, k, :], k == 0, k == 7)
                S.cp("vector", KT[:, g0:g0 + 512], PS[b][:, :])
                for cc in range(4):
                    b = nb(4, 8)
                    for k in range(8):
                        S.mm(PS[b][:, 0:256], h[:, k, cc * 128:(cc + 1) * 128], WO[:, k, 128:384], k == 0, k == 7)
                    S.cp("scalar" if cc % 2 else "vector", st[:, cc, :], PS[b][:, 0:256])
                blk0 = g0 // 128
                S.cp("gpsimd", V[:, blk0:blk0 + 4, :], st[:, :, 128:256])
                kd = sbk_p.h[l].rearrange("(c p) f -> p c f", p=128)[:, blk0:blk0 + 4, hh * 128:(hh + 1) * 128]
                vd = sbv_p.h[l].rearrange("(c p) f -> p c f", p=128)[:, blk0:blk0 + 4, hh * 128:(hh + 1) * 128]
                base = ((l * 8192 + g0) * 2 + hh) * 512 * 128
                S.dma("sync", DA(sbk_p, kd, base, base + 1), st[:, :, 0:128])
                S.dma("sync", DA(sbv_p, vd, base, base + 1), st[:, :, 128:256])
        if STOP == "s2":
            return
        tiles = []
        for Q in range(NQ):
            ob_b = 4 + (Q % 2)
            for kb in range(4 * Q + 3, -1, -1):
                m = kb - 4 * Q
                T = dict(nk=128, W=512,
                         zmm=[(0, 512, KT[:, kb * 128:(kb + 1) * 128], QT[:, Q * 512:(Q + 1) * 512])],
                         mask=[(0, 512, maskm(m))] if m >= 0 else [],
                         first=(kb == 4 * Q + 3), last=(kb == 0), ri=(4 * Q + 3 - kb),
                         av=[(PS[ob_b][:, :], V[:, kb, :], 0, 512)], post=None)
                if kb == 0:
                    def post(Q=Q, ob_b=ob_b):
                        o = oT[Q % 2]
                        S.cp("vector", o[:, :], PS[ob_b][:, :])
                        tq = Q // 4
                        S.dma("sync", ob[tq * 256 + hh * 128:tq * 256 + (hh + 1) * 128, (Q % 4) * 512:(Q % 4 + 1) * 512], o[:, :])
                    T["post"] = post
                tiles.append(T)
        attn_pipeline(tiles, AB)

    OST = A0.alloc("OSTs", [128, 8, 64], BF16)
    A1.lo = A1.p = A0.p

    def sample_attn(l):
        P = A1.sub()
        WU = [P.alloc("WUs", [128, 8, 512], BF16) for _ in range(2)]
        QTs = P.alloc("QTs", [128, 8, 64], BF16)
        KTn = P.alloc("KTn", [128, 8, 64], BF16)
        Vn = P.alloc("Vn", [64, 1024], BF16)
        stg = [P.alloc("sstg", [64, 512], F32) for _ in range(2)]
        Kc = P.alloc("Kc", [128, 16, 256], BF16)
        KTc = P.alloc("KTc", [128, 2, 2048], BF16)
        Vc = P.alloc("Vc", [128, 16, 256], BF16)
        AB = attn_bufs(P)
        for u in range(6):
            W = WU[u % 2]
            S.dma("gpsimd", W[:, :, :], RO(w_in.h[l][:, u * 512:(u + 1) * 512].rearrange("(k p) n -> p k n", p=128)))
            if u < 4:
                for c in range(4):
                    head = (u % 2) * 4 + c
                    b = nb(4, 8)
                    for k in range(8):
                        S.mm(PS[b][:, 0:64], W[:, k, c * 128:(c + 1) * 128], HTs[:, k, :], k == 0, k == 7)
                    if u < 2:
                        S.act(QTs[:, head, :], PS[b][:, 0:64], AF.Copy, scale=128 ** -0.5)
                    else:
                        S.cp("vector", KTn[:, head, :], PS[b][:, 0:64])
            if u >= 2:
                b = nb(4, 8)
                for k in range(8):
                    S.mm(PS[b][0:64, :], HTs[:, k, :], W[:, k, :], k == 0, k == 7)
                st = stg[u % 2]
                S.cp("scalar", st[:, :], PS[b][0:64, :])
                cu = (u % 2) * 512
                S.dma("sync", (sbk_s if u < 4 else sbv_s)[l, :, cu:cu + 512], st[:, :])
                if u >= 4:
                    S.cp("vector", Vn[:, cu:cu + 512], st[:, :])
        if STOP == "sqkv":
            return
        for hg in range(4):
            S.dma("gpsimd", Kc[:, :, :], RO(csk.h[l][:, hg * 256:(hg + 1) * 256].rearrange("(b p) f -> p b f", p=128)))
            S.dma("gpsimd", Vc[:, :, :], RO(csv.h[l][:, hg * 256:(hg + 1) * 256].rearrange("(b p) f -> p b f", p=128)))
            for hh in range(2):
                for b4 in range(4):
                    b = nb(4, 8)
                    for i in range(4):
                        blk = b4 * 4 + i
                        S.tr(PSB[b][:, i * 128:(i + 1) * 128], Kc[:, blk, hh * 128:(hh + 1) * 128], ident)
                    S.cp("vector" if b4 % 2 else "scalar", KTc[:, hh, b4 * 512:(b4 + 1) * 512], PSB[b][:, 0:512])
            tiles = []
            ob_b = 4 + (hg % 2)
            for kb in range(16, -1, -1):
                new = (kb == 16)
                nk = 64 if new else 128
                zmm, av, mask = [], [], []
                for hh in range(2):
                    head = 2 * hg + hh
                    c0, c1 = hh * 64, hh * 64 + 64
                    if new:
                        zmm.append((c0, c1, KTn[:, head, :], QTs[:, head, :]))
                        av.append((PS[ob_b][:, c0:c1], Vn[0:64, head * 128:(head + 1) * 128], c0, c1))
                        mask.append((c0, c1, maskm(0, 64, 64)))
                    else:
                        zmm.append((c0, c1, KTc[:, hh, kb * 128:(kb + 1) * 128], QTs[:, head, :]))
                        av.append((PS[ob_b][:, c0:c1], Vc[:, kb, hh * 128:(hh + 1) * 128], c0, c1))
                T = dict(nk=nk, W=128, zmm=zmm, mask=mask, first=new, last=(kb == 0), ri=16 - kb, av=av, post=None, ob=ob_b)
                if kb == 0:
                    def post(hg=hg, ob_b=ob_b):
                        S.cp("vector", OST[:, 2 * hg:2 * hg + 2, :], rv(PS[ob_b][:, 0:128], "p (h t) -> p h t", h=2))
                    T["post"] = post
                tiles.append(T)
            attn_pipeline(tiles, AB)

    class WStream:
        def __init__(s):
            s.specs = []
            s.collect = True
            s.bufs = None
            s.issued = 0
            s.used = 0

        def next(s, spec, hold=0):
            if s.collect:
                s.specs.append(spec)
                return s.bufs[0]
            assert s.specs[s.used][1:] == spec[1:]
            while s.issued < min(len(s.specs), s.used + len(s.bufs) - hold):
                ap, kc, ncols = s.specs[s.issued]
                buf = s.bufs[s.issued % len(s.bufs)]
                S.dma("gpsimd", buf[:, 0:kc, 0:ncols], RO(ap.rearrange("(k p) n -> p k n", p=128)))
                s.issued += 1
            buf = s.bufs[s.used % len(s.bufs)]
            s.used += 1
            return buf

    def s4(l, ws):
        P = A1.sub()
        ws.bufs = [P.alloc("WU", [128, 8, 512], BF16) for _ in range(2)]
        HT = P.alloc("HT4", [128, 8, 512], BF16)
        memKT = P.alloc("memKT", [128, 8, 256], BF16)
        memV = P.alloc("memV", [128, 2, 1024], BF16)
        LNG = P.alloc("LNG", [128, 1, 1024], F32)
        LNB = P.alloc("LNB", [128, 1, 1024], F32)
        BST = P.alloc("BST", [128, 1, 512], F32)
        WST = P.alloc("WST", [128, 512], BF16)
        TH = P.alloc("TH", [128, 4, 512], F32)
        sq = [P.alloc("sq4", [128, 512], BF16) for _ in range(2)]
        rstd = P.alloc("rstd4", [128, 512], F32)
        tmp = P.alloc("tmp4", [128, 512], F32)
        tmpx = [P.alloc("tmpx", [128, 512], F32) for _ in range(2)]
        st6 = P.alloc("st6", [128, 12], F32)
        mv = P.alloc("mv", [128, 2], F32)
        rs1 = P.alloc("rs1", [128, 2], F32)
        outT = P.alloc("outT", [128, 8, 512], F32)
        CA = TT(("s",), nc.alloc_sbuf_tensor_at(uniq("CA"), [128, 2, 8, 512], BF16, offset=outT.base), outT.base, [128, 2, 8, 512], BF16)
        RB = P.sub()
        osb = RB.alloc("osb", [128, 8, 512], BF16)
        osg = RB.alloc("osg", [128, 8, 512], BF16)
        omem = RB.alloc("omem", [128, 8, 512], BF16)
        RB2 = Arena(RB.lo, RB.hi)
        actT = RB2.alloc("actT", [128, 22, 512], BF16)
        RA0 = max(RB.p, RB2.p)
        RA = Arena(RA0, P.hi)
        gu = RA.alloc("gu", [128, 8, 512], BF16)
        gv = [RA.alloc("gv", [128, 1024], F32) for _ in range(2)]
        vnb = RA.alloc("vnb", [128, 4, 1024], BF16)
        scr = [RA.alloc("scr", [128, 512], F32) for _ in range(3)]
        RAb = Arena(RA0, P.hi)
        qm = RAb.alloc("qm", [128, 8, 512], BF16)
        PT = [RAb.alloc("PT", [128, 2, 512], BF16) for _ in range(2)]
        rden = RAb.alloc("rden", [128, 512], F32)
        RAc = Arena(RA0, P.hi)
        M = RAc.alloc("M", [128, 8, 512], F32)
        Mb = RAc.alloc("Mb", [128, 8, 512], BF16)
        RAm = Arena(RA0, P.hi)
        mst = RAm.alloc("mst", [128, 2, 1024], F32)
        memT = RAm.alloc("memT", [128, 8, 256], F32)
        hmT = RAm.alloc("hmT", [128, 8, 256], BF16)
        ost = [RAm.alloc("ost", [128, 512], F32) for _ in range(2)]
        Kcm = RAm.alloc("Kcm", [128, 2, 1024], BF16)
        WSF = RAm.alloc("WSF", [128, 512], F32)
        rowt = tmpx[0]

        for (dst_, src_, n_) in ((LNG, lnp.h[l, 0:1, :], 1024), (LNB, lnp.h[l, 1:2, :], 1024), (BST, bsp.h[l:l + 1, :], 512)):
            for hf in range(n_ // 512):
                S.dma("sync", rowt[0:1, :], RO(src_[:, hf * 512:(hf + 1) * 512]))
                b = nb(0, 7)
                S.mm(PS[b][:, :], ONES1[0:1, :], rowt[0:1, :], True, True)
                S.cp("vector", dst_[:, 0, hf * 512:(hf + 1) * 512], PS[b][:, :])
        S.dma("sync", WSF[:, :], RO(wsT.h[l]))
        for g in range(4):
            S.tt("vector", WST[:, g * 128:(g + 1) * 128], WSF[:, g * 128:(g + 1) * 128], mincl, ALU.mult)

        def gelu(ps, out, T, npart=128):
            x2 = scr[0][0:npart, 0:T]
            inner = scr[1][0:npart, 0:T]
            xh = scr[2][0:npart, 0:T]
            S.act(x2, ps, AF.Square)
            S.ts("vector", x2, x2, GC2, 1.0, ALU.mult, ALU.add)
            S.tt("vector", inner, x2, ps, ALU.mult)
            S.act(inner, inner, AF.Tanh, scale=GC1)
            S.act(xh, ps, AF.Copy, scale=0.5)
            S.stt(out, inner, 1.0, xh, ALU.add, ALU.mult)

        S.dma("sync", mst[:, :, :], RO(memp.h.rearrange("(b p) f -> p b f", p=128)))
        for mb in range(2):
            for half in range(2):
                b = nb(0, 7)
                for kk in range(4):
                    k = half * 4 + kk
                    S.tr(PS[b][:, kk * 128:(kk + 1) * 128], mst[:, mb, k * 128:(k + 1) * 128], CF[:, :])
                S.cp("vector", memT[:, half * 4:half * 4 + 4, mb * 128:(mb + 1) * 128],
                     rv(PS[b][:, :], "p (k t) -> p k t", k=4))
        rms_fm(lambda k: memT[:, k, :], lambda k: gcol(l, 4, k), lambda k: hmT[:, k, :], 256,
               [s_[:, 0:256] for s_ in sq], rstd[:, 0:256], tmp[:, 0:256])
        for u in range(4):
            W = ws.next((w_mkv.h[l][:, u * 512:(u + 1) * 512], 8, 512))
            if u < 2:
                for c4 in range(4):
                    c = u * 4 + c4
                    b = nb(0, 7)
                    for k in range(8):
                        S.mm(PS[b][:, 0:256], W[:, k, c4 * 128:(c4 + 1) * 128], hmT[:, k, :], k == 0, k == 7)
                    S.cp("vector", memKT[:, c, :], PS[b][:, 0:256])
            for mb in range(2):
                b = nb(0, 7)
                for k in range(8):
                    S.mm(PS[b][:, :], hmT[:, k, mb * 128:(mb + 1) * 128], W[:, k, :], k == 0, k == 7)
                st = ost[mb]
                S.cp("scalar", st[:, :], PS[b][:, :])
                cu = (u % 2) * 512
                S.dma("sync", (mk_p if u < 2 else mv_p)[l, mb * 128:(mb + 1) * 128, cu:cu + 512], st[:, :])
                if u >= 2:
                    S.cp("vector", memV[:, mb, cu:cu + 512], st[:, :])

        import os
        NT = int(os.environ.get('S4_TILES', '5'))
        PARTS = os.environ.get('S4_PARTS', 'abcdfgh')
        for ti in range(5):
            if ti >= NT:
                break
            sample = (ti == 4)
            T = 64 if sample else 512
            TP = 64 if sample else 128
            NCC = 1 if sample else 4
            if sample:
                X = lambda k: XTs[:, k, :]
                hT = lambda k: HTs[:, k, :]
                hTc = lambda k, cc: HTs[:, k, :]
                S.dma("gpsimd", Kcm[:, :, :], RO(cmk.h[l].rearrange("(b p) f -> p b f", p=128)))
                S.dma("gpsimd", memV[:, :, :], RO(cmv.h[l].rearrange("(b p) f -> p b f", p=128)))
                for c in range(8):
                    b = nb(0, 7)
                    for mb in range(2):
                        S.tr(PSB[b][:, mb * 128:(mb + 1) * 128], Kcm[:, mb, c * 128:(c + 1) * 128], ident)
                    S.cp("vector", memKT[:, c, :], PSB[b][:, 0:256])
                osb_k = lambda k: OST[:, k, :]
            else:
                X = lambda k, ti=ti: XT[:, k, ti * 512:(ti + 1) * 512]
                hT = lambda k: HT[:, k, :]
                hTc = lambda k, cc: HT[:, k, cc * 128:(cc + 1) * 128]
                S.dma("sync", HT[:, :, :], rv(hb[ti * 1024:(ti + 1) * 1024, :], "(k p) n -> p k n", p=128))

                ocand = oall.h.rearrange("(r k p) n -> r p k n", r=4, p=128)

                def cand_dma(r, ti=ti):
                    if os.environ.get("NO_CAND"):
                        return
                    S.dma("sync", CA[:, r % 2, :, :], Acc(ocand[r][:, :, ti * 512:(ti + 1) * 512], oall.space, 0, oall.whole().hi))

                def cand_sel(r):
                    if os.environ.get("NO_CAND"):
                        return
                    if os.environ.get("ZERO_OSB"):
                        if r == 0:
                            S.add("vector", lambda e: e.memset(osb[:, :, :].ap, 0.0), [], [osb[:, :, :]])
                        return
                    if r == 0:
                        S.ts("vector", osb[:, :, :], CA[:, 0, :, :], SEL[:, 0:1], None, ALU.mult)
                    else:
                        S.stt(osb[:, :, :], CA[:, r % 2, :, :], SEL[:, r:r + 1], osb[:, :, :], ALU.mult, ALU.add)
                cand_dma(0)
                cand_dma(1)
                osb_k = lambda k: osb[:, k, :]

            if 'a' not in PARTS:
                continue
            for uu in range(2):
                W = ws.next((w_in.h[l][:, 3072 + uu * 512:3072 + (uu + 1) * 512], 8, 512))
                for c4 in range(4):
                    c = uu * 4 + c4
                    b = nb(0, 7)
                    for k in range(8):
                        S.mm(PS[b][:, 0:T], W[:, k, c4 * 128:(c4 + 1) * 128], hT(k), k == 0, k == 7)
                    gelu(PS[b][:, 0:T], gu[:, c, 0:T], T)
            if not sample:
                cand_sel(0)
                cand_sel(1)
                cand_dma(2)
                cand_dma(3)
            if 'b' not in PARTS:
                continue
            Wv = [ws.next((w_in.h[l][:, 4096 + half * 512:4096 + (half + 1) * 512], 8, 512), hold=half) for half in range(2)]
            for cc in range(NCC):
                gvc = gv[cc % 2]
                for half in range(2):
                    b = nb(0, 7)
                    for k in range(8):
                        S.mm(PS[b][0:TP, :], hTc(k, cc), Wv[half][:, k, :], k == 0, k == 7)
                    gelu(PS[b][0:TP, :], gvc[0:TP, half * 512:(half + 1) * 512], 512, TP)
                g_ = gvc[0:TP, :]
                for hf in range(2):
                    S.add("vector", lambda e, hf=hf, gvc=gvc, TP=TP: e.bn_stats(out=st6[0:TP, hf * 6:(hf + 1) * 6].ap,
                                                                         in_=gvc[0:TP, hf * 512:(hf + 1) * 512].ap),
                          [gvc[0:TP, hf * 512:(hf + 1) * 512]], [st6[0:TP, hf * 6:(hf + 1) * 6]])
                S.add("vector", lambda e, TP=TP: e.bn_aggr(out=mv[0:TP, :].ap, in_=st6[0:TP, :].ap), [st6[0:TP, :]], [mv[0:TP, :]])
                S.act(rs1[0:TP, 0:1], mv[0:TP, 1:2], AF.Ln, bias=EPS)
                S.act(rs1[0:TP, 1:2], rs1[0:TP, 0:1], AF.Exp, scale=-0.5)
                if not os.environ.get("NO_LN"):
                    S.ts("vector", g_, g_, mv[0:TP, 0:1], rs1[0:TP, 1:2], ALU.subtract, ALU.mult)
                S.tt("vector", g_, g_, LNG[0:TP, 0, :], ALU.mult)
                S.tt("vector", g_, g_, LNB[0:TP, 0, :], ALU.add)
                S.cp("scalar", vnb[0:TP, cc, :], g_)
                if sample:
                    S.dma("sync", sgv_s[l, :, :], gvc[0:64, :])
            if 'c' not in PARTS:
                continue
            for cc in range(NCC):
                for c in range(8):
                    g = c // 2
                    b = nb(0, 7)
                    S.mm(PS[b][:, 0:TP], vnb[0:TP, cc, c * 128:(c + 1) * 128], WST[0:TP, g * 128:g * 128 + TP], True, True)
                    t_ = tmpx[c % 2][:, 0:TP]
                    S.tt("vector", t_, PS[b][:, 0:TP], BST[:, 0, g * 128:g * 128 + TP], ALU.add)
                    if os.environ.get("OSG_NOGU"):
                        S.cp("vector", osg[:, c, cc * TP:(cc + 1) * TP], t_)
                    elif os.environ.get("OSG_ONLYGU"):
                        S.cp("vector", osg[:, c, cc * TP:(cc + 1) * TP], gu[:, c, cc * TP:(cc + 1) * TP])
                    else:
                        S.tt("vector", osg[:, c, cc * TP:(cc + 1) * TP], t_, gu[:, c, cc * TP:(cc + 1) * TP], ALU.mult)
            if 'd' not in PARTS:
                continue
            for uu in range(2):
                W = ws.next((w_in.h[l][:, 5120 + uu * 512:5120 + (uu + 1) * 512], 8, 512))
                for c4 in range(4):
                    c = uu * 4 + c4
                    b = nb(0, 7)
                    for k in range(8):
                        S.mm(PS[b][:, 0:T], W[:, k, c4 * 128:(c4 + 1) * 128], hT(k), k == 0, k == 7)
                    S.act(qm[:, c, 0:T], PS[b][:, 0:T], AF.Copy, scale=1.0 / 16)
            for h in range(4):
                pt = PT[h % 2]
                for mb in range(2):
                    b = nb(0, 7)
                    for dc in range(2):
                        S.mm(PS[b][:, 0:T], memKT[:, 2 * h + dc, mb * 128:(mb + 1) * 128], qm[:, 2 * h + dc, 0:T], dc == 0, dc == 1)
                    S.act(pt[:, mb, 0:T], PS[b][:, 0:T], AF.Exp)
                b = nb(0, 7)
                for mb in range(2):
                    S.mm(PS[b][:, 0:T], ones, pt[:, mb, 0:T], mb == 0, mb == 1)
                rd_ = rden[:, 0:T]
                pb_ = PS[b][:, 0:T]
                S.add("vector", lambda e, rd_=rd_, pb_=pb_: e.reciprocal(out=rd_.ap, in_=pb_.ap), [pb_], [rd_])
                for dc in range(2):
                    b = nb(0, 7)
                    for mb in range(2):
                        S.mm(PS[b][:, 0:T], memV[:, mb, (2 * h + dc) * 128:(2 * h + dc + 1) * 128], pt[:, mb, 0:T], mb == 0, mb == 1)
                    S.tt("vector", omem[:, 2 * h + dc, 0:T], PS[b][:, 0:T], rd_, ALU.mult)
                    if os.environ.get("ZERO_OMEM"):
                        om_ = omem[:, 2 * h + dc, 0:T]
                        S.add("vector", lambda e, om_=om_: e.memset(om_.ap, 0.0), [], [om_])
            if not sample:
                cand_sel(2)
                cand_sel(3)
            if os.environ.get("ZERO_OSG"):
                S.add("vector", lambda e: e.memset(osg[:, :, :].ap, 0.0), [], [osg[:, :, :]])
            if 'f' not in PARTS:
                continue
            for i, ok in enumerate([osb_k, (lambda k: osg[:, k, 0:T]), (lambda k: omem[:, k, 0:T])]):
                for half in range(2):
                    Wg = ws.next((w_in.h[l][:, 6144 + i * 1024 + half * 512:6144 + i * 1024 + (half + 1) * 512], 8, 512))
                    for c4 in range(4):
                        c = half * 4 + c4
                        b = nb(0, 7)
                        for k in range(8):
                            S.mm(PS[b][:, 0:T], Wg[:, k, c4 * 128:(c4 + 1) * 128], hT(k), k == 0, k == 7)
                        S.act(TH[:, c4, 0:T], PS[b][:, 0:T], AF.Tanh, bias=BGH[:, l, i * 8 + c:i * 8 + c + 1], scale=0.5)
                    Wb = ws.next((w_br.h[l, i][:, half * 512:(half + 1) * 512], 8, 512))
                    for c4 in range(4):
                        c = half * 4 + c4
                        b = nb(0, 7)
                        for k in range(8):
                            a_ = ok(k)
                            S.mm(PS[b][:, 0:T], Wb[:, k, c4 * 128:(c4 + 1) * 128], a_ if sample or i else osb[:, k, 0:T], k == 0, k == 7)
                        if i == 0:
                            S.stt(M[:, c, 0:T], TH[:, c4, 0:T], 1.0, PS[b][:, 0:T], ALU.add, ALU.mult)
                        else:
                            t_ = tmpx[c % 2][:, 0:T]
                            S.stt(t_, TH[:, c4, 0:T], 1.0, PS[b][:, 0:T], ALU.add, ALU.mult)
                            S.tt("vector", M[:, c, 0:T], M[:, c, 0:T], t_, ALU.add)
            for c in range(8):
                S.act(Mb[:, c, 0:T], M[:, c, 0:T], AF.Copy, scale=0.5)

            def postnorm(which):
                S.cp("vector", sq[0][:, 0:T], TH[:, 1, 0:T])
                S.mm(PS[7][:, 0:T], ones, sq[0][:, 0:T], True, True)
                rstd_from_ss(PS[7][:, 0:T], rstd[:, 0:T], tmp[:, 0:T], T)
                for c in range(8):
                    t_ = tmpx[c % 2][:, 0:T]
                    S.stt(t_, outT[:, c, 0:T], gcol(l, which, c), rstd[:, 0:T], ALU.mult, ALU.mult)
                    S.tt("vector", X(c), X(c), t_, ALU.add)

            def evac(c, b):
                S.cp("vector", outT[:, c, 0:T], PS[b][:, 0:T])
                if c == 0:
                    S.act(TH[:, 1, 0:T], outT[:, c, 0:T], AF.Square)
                else:
                    S.act(TH[:, 0, 0:T], outT[:, c, 0:T], AF.Square)
                    S.tt("vector", TH[:, 1, 0:T], TH[:, 1, 0:T], TH[:, 0, 0:T], ALU.add)

            if 'g' not in PARTS:
                continue
            for half in range(2):
                W = ws.next((w_o.h[l][:, half * 512:(half + 1) * 512], 8, 512))
                for c4 in range(4):
                    c = half * 4 + c4
                    b = nb(0, 7)
                    for k in range(8):
                        S.mm(PS[b][:, 0:T], W[:, k, c4 * 128:(c4 + 1) * 128], Mb[:, k, 0:T], k == 0, k == 7)
                    evac(c, b)
            postnorm(1)
            if STOP == "mix":
                continue
            if 'h' not in PARTS:
                continue
            rms_fm(X, lambda k: gcol(l, 2, k), lambda k: HT[:, k, 0:T], T,
                   [s_[:, 0:T] for s_ in sq], rstd[:, 0:T], tmp[:, 0:T])
            for u in range(11):
                W = ws.next((w_f1.h[l][:, u * 512:(u + 1) * 512], 8, 512))
                for jj in range(2):
                    j = u * 2 + jj
                    ba = nb(0, 7)
                    for k in range(8):
                        S.mm(PS[ba][:, 0:T], W[:, k, jj * 256:jj * 256 + 128], HT[:, k, 0:T], k == 0, k == 7)
                    bb = nb(0, 7)
                    for k in range(8):
                        S.mm(PS[bb][:, 0:T], W[:, k, jj * 256 + 128:jj * 256 + 256], HT[:, k, 0:T], k == 0, k == 7)
                    th_ = TH[:, j % 4, 0:T]
                    t_ = tmpx[j % 2][:, 0:T]
                    S.act(th_, PS[ba][:, 0:T], AF.Tanh, scale=0.5)
                    S.stt(t_, th_, 1.0, PS[ba][:, 0:T], ALU.add, ALU.mult)
                    S.stt(actT[:, j, 0:T], t_, 0.5, PS[bb][:, 0:T], ALU.mult, ALU.mult)
            for half in range(2):
                for kg in range(3):
                    k0 = kg * 8
                    kc = 8 if kg < 2 else 6
                    W = ws.next((w_f2.h[l][k0 * 128:(k0 + kc) * 128, half * 512:(half + 1) * 512], kc, 512))
                    for c4 in range(4):
                        for kk in range(kc):
                            j = k0 + kk
                            S.mm(PS[3 + c4][:, 0:T], W[:, kk, c4 * 128:(c4 + 1) * 128], actT[:, j, 0:T], j == 0, j == 21)
                for c4 in range(4):
                    evac(half * 4 + c4, 3 + c4)
            postnorm(3)

    def store_y():
        P = A1.sub()
        stg = [P.alloc("ystg", [128, 1024], F32) for _ in range(2)]
        for c in range(17):
            n = 128 if c < 16 else 64
            st = stg[c % 2]
            for half in range(2):
                b = nb(0, 7)
                for kk in range(4):
                    k = half * 4 + kk
                    src = XT[:, k, c * 128:(c + 1) * 128] if c < 16 else XTs[:, k, :]
                    S.tr(PS[b][0:n, kk * 128:(kk + 1) * 128], src, CF[:, :])
                S.cp("vector" if half else "scalar", st[0:n, half * 512:(half + 1) * 512], PS[b][0:n, :])
            S.dma("sync", y_p[c * 128:(c + 1) * 128, :] if c < 16 else y_s[:, :], st[0:n, :])

    load_x()
    for l in range(DEPTH):
        s1(l)
        if STOP == "s1":
            break
        for hh in range(2):
            s23(l, hh)
        if STOP == "s2":
            break
        import os
        if not os.environ.get("SKIP_OG"):
            for tq in range(4):
                allgather(ob, oall, tq, 256)
        if not os.environ.get("SKIP_SA"):
            sample_attn(l)
        if STOP in ("s3", "sqkv"):
            break
        ws = WStream()
        S.dry = True
        ws.collect = True
        s4(l, ws)
        S.dry = False
        ws.collect = False
        s4(l, ws)
        if STOP == "l0" or os.environ.get("ONE_LAYER"):
            break
    store_y()
    S.wait_all("sync", [o.whole() for o in outs_all])
    stats = S.run_block()
    return nc, stats


_CACHE = {}


def _colvec(v):
    return np.ascontiguousarray(np.asarray(v, np.float32).reshape(-1, 128).T)


def kernel(x_prompt, x_sample, cache_sb_k, cache_sb_v, cache_mem_k, cache_mem_v, mem_prompt,
           g_pre_mix, w_in, b_gate, ln_sg_g, ln_sg_b, w_spatial, b_spatial, g_mem, w_mem_kv,
           w_branch, w_out, g_post_mix, g_pre_ffn, w_ffn_in, w_ffn_out, g_post_ffn):
    f = lambda a: np.ascontiguousarray(np.asarray(a, np.float32))
    x_prompt, x_sample = f(x_prompt), f(x_sample)
    w_in = f(w_in)
    L = DEPTH
    cols = np.zeros((L, 128, 64), np.float32)
    for l in range(L):
        for i, g in enumerate((g_pre_mix, g_post_mix, g_pre_ffn, g_post_ffn, g_mem)):
            cols[l, :, i * 8:(i + 1) * 8] = _colvec(np.asarray(g)[l])
        cols[l, :, 40:64] = _colvec(np.asarray(b_gate)[l])
    lnp = f(np.stack([np.asarray(ln_sg_g), np.asarray(ln_sg_b)], axis=1))
    wsT = f(np.asarray(w_spatial).transpose(0, 3, 1, 2).reshape(L, 128, 512))
    bsp = f(np.asarray(b_spatial).reshape(L, 512))
    wf = np.asarray(w_ffn_in, np.float32)
    dff = wf.shape[2] // 2
    w_f1 = f(np.stack([wf[:, :, :dff].reshape(L, 1024, 22, 128), wf[:, :, dff:].reshape(L, 1024, 22, 128)], axis=3)
             .reshape(L, 1024, 2 * dff))
    cb, cf = host_consts()
    shared = dict(w_in=w_in, w_mkv=f(w_mem_kv), w_br=f(w_branch), w_o=f(w_out), w_f1=w_f1, w_f2=f(w_ffn_out),
                  cols=cols, lnp=lnp, wsT=wsT, bsp=bsp, cb=cb, cf=cf)
    csk = np.asarray(cache_sb_k, np.float32)
    csv = np.asarray(cache_sb_v, np.float32)
    cmk = np.asarray(cache_mem_k, np.float32)
    cmv = np.asarray(cache_mem_v, np.float32)
    in_maps = []
    for c in range(8):
        b, j = c // 4, c % 4
        w_own = np.zeros((L, 2, 1024, 384), np.float32)
        for hh in range(2):
            h = 2 * j + hh
            for i in range(3):
                w_own[:, hh, :, i * 128:(i + 1) * 128] = w_in[:, :, i * 1024 + h * 128:i * 1024 + (h + 1) * 128]
        m = dict(shared)
        m.update(xp=f(x_prompt[b, j * 2048:(j + 1) * 2048]), xs=f(x_sample[c]),
                 csk=f(csk[:, c].reshape(L, 2048, 1024)), csv=f(csv[:, c].reshape(L, 2048, 1024)),
                 cmk=f(cmk[:, c].reshape(L, 256, 1024)), cmv=f(cmv[:, c].reshape(L, 256, 1024)),
                 memp=f(np.asarray(mem_prompt, np.float32)[b]), w_own=w_own,
                 sel=np.ascontiguousarray(np.tile(np.eye(4, dtype=np.float32)[j][None, :], (128, 1))))
        in_maps.append(m)
    if "nc" not in _CACHE:
        _CACHE["nc"] = build()[0]
    res = run_bass_kernel_spmd(_CACHE["nc"], in_maps, core_ids=list(range(8)))
    R = res.results
    y_prompt = np.zeros((2, 8192, 1024), np.float32)
    y_sample = np.zeros((8, 64, 1024), np.float32)
    sbk_p = np.zeros((L, 2, 8192, 8, 128), np.float32)
    sbv_p = np.zeros((L, 2, 8192, 8, 128), np.float32)
    mk_p = np.zeros((L, 2, 256, 4, 256), np.float32)
    mv_p = np.zeros((L, 2, 256, 4, 256), np.float32)
    sbk_s = np.zeros((L, 8, 64, 8, 128), np.float32)
    sbv_s = np.zeros((L, 8, 64, 8, 128), np.float32)
    sgv_s = np.zeros((L, 8, 64, 1024), np.float32)
    for c in range(8):
        b, j = c // 4, c % 4
        r = R[c]
        y_prompt[b, j * 2048:(j + 1) * 2048] = r["y_p"]
        y_sample[c] = r["y_s"]
        sbk_p[:, b, :, 2 * j:2 * j + 2, :] = np.asarray(r["sbk_p"]).reshape(L, 8192, 2, 128)
        sbv_p[:, b, :, 2 * j:2 * j + 2, :] = np.asarray(r["sbv_p"]).reshape(L, 8192, 2, 128)
        if j == 0:
            mk_p[:, b] = np.asarray(r["mk_p"]).reshape(L, 256, 4, 256)
            mv_p[:, b] = np.asarray(r["mv_p"]).reshape(L, 256, 4, 256)
        sbk_s[:, c] = np.asarray(r["sbk_s"]).reshape(L, 64, 8, 128)
        sbv_s[:, c] = np.asarray(r["sbv_s"]).reshape(L, 64, 8, 128)
        sgv_s[:, c] = r["sgv_s"]
    return (y_prompt, y_sample, sbk_p, sbv_p, mk_p, mv_p, sbk_s, sbv_s, sgv_s)
```
